# Optimizing a Trainium2 kernel written in Bass

```python
import jax, jax.numpy as jnp
from jax import lax
import numpy as np


D_MODEL = 2048
BATCH = 2
SEQ = 8192
DEPTH = 4

N_MIXERS = 3
PLE_DIM = 256
EPS = 1e-6
NEG = -1e30
BIG = 1e4

SB_HEADS = 16
SB_HEAD_DIM = D_MODEL // SB_HEADS
SB_WIDTH = SB_HEADS * SB_HEAD_DIM
SB_QBLOCK = 128

NSA_HEADS = 16
NSA_KV_GROUPS = 4
NSA_HEAD_DIM = D_MODEL // NSA_HEADS
NSA_WIDTH = NSA_HEADS * NSA_HEAD_DIM
NSA_CMP_LEN = 32
NSA_CMP_STRIDE = 16
NSA_SEL_LEN = 64
NSA_SEL_TOPK = 16
NSA_WINDOW = 512
NSA_QBLOCK = 64
NSA_N_BRANCH = 3
NSA_IN_DIM = (NSA_HEADS * NSA_HEAD_DIM + 6 * NSA_KV_GROUPS * NSA_HEAD_DIM
              + NSA_N_BRANCH * NSA_HEADS + NSA_WIDTH)

GM_GROUPS = 16
GM_CHUNK = 128
GM_WIDTH = D_MODEL
GM_GROUP_DIM = GM_WIDTH // GM_GROUPS

N_A = (DEPTH + 2) // 3
N_B = (DEPTH + 1) // 3
N_C = DEPTH // 3

kernel_name = "hybrid_sb_nsa_gmlp_decoder"


def rms_norm(x, g):
    xf = x.astype(jnp.float32)
    y = xf * lax.rsqrt(jnp.mean(xf * xf, axis=-1, keepdims=True) + EPS)
    return (y * g.astype(jnp.float32)).astype(x.dtype)


def alibi_slopes(n):
    return jnp.asarray(np.exp2(-8.0 * np.arange(1, n + 1) / n).astype(np.float32))


def stick_breaking_attention(q, k, v):
    B, S, H, Dh = q.shape
    nblk = S // SB_QBLOCK
    scale = Dh ** -0.5
    kf = k.astype(jnp.float32)
    vf = v.astype(jnp.float32)
    qb = q.reshape(B, nblk, SB_QBLOCK, H, Dh).transpose(1, 0, 2, 3, 4)
    key_pos = jnp.arange(S)

    def block(args):
        q_blk, blk = args
        t = blk * SB_QBLOCK + jnp.arange(SB_QBLOCK)
        z = jnp.einsum('bqhd,bshd->bhqs', q_blk.astype(jnp.float32), kf) * scale
        mask = key_pos[None, :] < t[:, None]
        log_1m = jnp.where(mask, jax.nn.log_sigmoid(-z), 0.0)
        tail = lax.cumsum(log_1m, axis=3, reverse=True) - log_1m
        a = jnp.where(mask, jnp.exp(jax.nn.log_sigmoid(z) + tail), 0.0)
        return jnp.einsum('bhqs,bshd->bqhd', a, vf)

    out = lax.map(block, (qb, jnp.arange(nblk)))
    return out.transpose(1, 0, 2, 3, 4).reshape(B, S, H * Dh)


def sb_mixer(h, w_in, w_out):
    B, S, _ = h.shape
    q, k, v, z = jnp.split(h @ w_in, 4, axis=-1)
    shp = (B, S, SB_HEADS, SB_HEAD_DIM)
    o = stick_breaking_attention(q.reshape(shp), k.reshape(shp), v.reshape(shp))
    return (o.astype(h.dtype) * jax.nn.silu(z)) @ w_out


def compress_blocks(kv, blk_idx, pos, w1, w2):
    B, S, G, Dh = kv.shape
    blocks = kv[:, blk_idx] + pos[:, None, :]
    n_cmp = blocks.shape[1]
    flat = blocks.transpose(0, 1, 3, 2, 4).reshape(B, n_cmp, G, NSA_CMP_LEN * Dh)
    return jax.nn.gelu(flat @ w1, approximate=False) @ w2


def nsa_mixer(h, w_in, pos_k, pos_v, ck_w1, ck_w2, cv_w1, cv_w2, w_out):
    B, S, _ = h.shape
    H, G, Dh = NSA_HEADS, NSA_KV_GROUPS, NSA_HEAD_DIM
    HG = H // G
    kvw = G * Dh
    sizes = [H * Dh, kvw, kvw, kvw, kvw, kvw, kvw, NSA_N_BRANCH * H]
    offs = np.cumsum(sizes).tolist()
    q, k_c, v_c, k_s, v_s, k_w, v_w, g, z = jnp.split(h @ w_in, offs, axis=-1)
    kv_shape = (B, S, G, Dh)

    n_cmp = (S - NSA_CMP_LEN) // NSA_CMP_STRIDE + 1
    blk_idx = np.arange(n_cmp)[:, None] * NSA_CMP_STRIDE + np.arange(NSA_CMP_LEN)[None, :]
    kc = compress_blocks(k_c.reshape(kv_shape), blk_idx, pos_k, ck_w1, ck_w2).astype(jnp.float32)
    vc = compress_blocks(v_c.reshape(kv_shape), blk_idx, pos_v, cv_w1, cv_w2).astype(jnp.float32)
    cmp_end = jnp.arange(n_cmp) * NSA_CMP_STRIDE + NSA_CMP_LEN - 1

    k_sel = k_s.reshape(kv_shape).transpose(0, 2, 1, 3).astype(jnp.float32)
    v_sel = v_s.reshape(kv_shape).transpose(0, 2, 1, 3).astype(jnp.float32)

    pad = ((0, 0), (NSA_WINDOW, 0), (0, 0), (0, 0))
    k_win = jnp.pad(k_w.reshape(kv_shape).astype(jnp.float32), pad)
    v_win = jnp.pad(v_w.reshape(kv_shape).astype(jnp.float32), pad)

    gates = jax.nn.sigmoid(g.astype(jnp.float32)).reshape(B, S, NSA_N_BRANCH, G, HG)
    slopes = alibi_slopes(H).reshape(G, HG)
    n_sel_blocks = S // NSA_SEL_LEN
    top_n = min(NSA_SEL_TOPK, n_sel_blocks)
    ratio = NSA_SEL_LEN // NSA_CMP_STRIDE
    cmp_pad = n_sel_blocks * ratio - n_cmp
    Q = NSA_QBLOCK
    n_blk = S // Q
    scale = Dh ** -0.5
    b_idx = jnp.arange(B)[:, None, None, None]
    g_idx = jnp.arange(G)[None, :, None, None]
    sel_off = jnp.arange(NSA_SEL_LEN)
    win_off = jnp.arange(NSA_WINDOW + Q)
    sel_j = jnp.arange(n_sel_blocks)

    def block(args):
        q_blk, g_blk, blk = args
        t = blk * Q + jnp.arange(Q)
        qf = q_blk.astype(jnp.float32) * scale

        dist_c = t[:, None] - cmp_end[None, :]
        mask_c = dist_c >= 0
        s_c = jnp.einsum('bqgnd,bcgd->bgnqc', qf, kc) - slopes[:, :, None, None] * dist_c.astype(jnp.float32)
        p_c = jax.nn.softmax(jnp.where(mask_c, s_c, NEG), axis=-1) * mask_c
        o_c = jnp.einsum('bgnqc,bcgd->bqgnd', p_c, vc)

        imp = jnp.pad(p_c.sum(axis=2), ((0, 0), (0, 0), (0, 0), (0, cmp_pad)))
        imp = imp.reshape(B, G, Q, n_sel_blocks, ratio).sum(-1)
        cur = (t // NSA_SEL_LEN)[:, None]
        forced = (sel_j == 0) | (sel_j == cur) | (sel_j == cur - 1)
        imp = jnp.where(forced, BIG, jnp.where(sel_j <= cur, imp, -BIG))
        _, sel = lax.top_k(imp, top_n)
        tok = (sel[..., None] * NSA_SEL_LEN + sel_off).reshape(B, G, Q, top_n * NSA_SEL_LEN)
        k_g = k_sel[b_idx, g_idx, tok]
        v_g = v_sel[b_idx, g_idx, tok]
        dist_s = t[None, None, :, None] - tok
        s_s = (jnp.einsum('bqgnd,bgqtd->bgnqt', qf, k_g)
               - slopes[None, :, :, None, None] * dist_s[:, :, None].astype(jnp.float32))
        mask_s = (dist_s >= 0)[:, :, None]
        p_s = jax.nn.softmax(jnp.where(mask_s, s_s, NEG), axis=-1)
        o_s = jnp.einsum('bgnqt,bgqtd->bqgnd', p_s, v_g)

        k_b = lax.dynamic_slice_in_dim(k_win, blk * Q, NSA_WINDOW + Q, axis=1)
        v_b = lax.dynamic_slice_in_dim(v_win, blk * Q, NSA_WINDOW + Q, axis=1)
        s_pos = blk * Q - NSA_WINDOW + win_off
        dist_w = t[:, None] - s_pos[None, :]
        mask_w = (dist_w >= 0) & (dist_w < NSA_WINDOW) & (s_pos[None, :] >= 0)
        s_w = jnp.einsum('bqgnd,bkgd->bgnqk', qf, k_b) - slopes[:, :, None, None] * dist_w.astype(jnp.float32)
        p_w = jax.nn.softmax(jnp.where(mask_w, s_w, NEG), axis=-1)
        o_w = jnp.einsum('bgnqk,bkgd->bqgnd', p_w, v_b)

        return (g_blk[:, :, 0, :, :, None] * o_c + g_blk[:, :, 1, :, :, None] * o_s
                + g_blk[:, :, 2, :, :, None] * o_w)

    qb = q.reshape(B, n_blk, Q, G, HG, Dh).transpose(1, 0, 2, 3, 4, 5)
    gb = gates.reshape(B, n_blk, Q, NSA_N_BRANCH, G, HG).transpose(1, 0, 2, 3, 4, 5)
    o = lax.map(block, (qb, gb, jnp.arange(n_blk)))
    o = o.transpose(1, 0, 2, 3, 4, 5).reshape(B, S, H * Dh)
    return (o.astype(h.dtype) * jax.nn.silu(z)) @ w_out


def gmlp_mixer(h, w_in, v_norm_g, w_s, b_s, w_out):
    B, S, _ = h.shape
    u, v, z = jnp.split(h @ w_in, 3, axis=-1)
    u = jax.nn.gelu(u, approximate=False)
    v = rms_norm(jax.nn.gelu(v, approximate=False), v_norm_g)
    n_chunk = S // GM_CHUNK
    vc = v.reshape(B, n_chunk, GM_CHUNK, GM_GROUPS, GM_GROUP_DIM)
    causal = jnp.tril(jnp.ones((GM_CHUNK, GM_CHUNK), dtype=bool))
    ws = jnp.where(causal, w_s, jnp.zeros((), w_s.dtype))
    mixed = jnp.einsum('gts,bnsgc->bntgc', ws, vc) + b_s.T[:, :, None]
    y = u * mixed.reshape(B, S, GM_WIDTH) * jax.nn.silu(z)
    return y @ w_out


def setup_inputs(seed: int = 0) -> dict:
    key = jax.random.key(seed)
    ks = jax.random.split(key, 24)
    f32 = jnp.float32

    def normal(k, shape, scale):
        return jax.random.normal(k, shape, f32) * scale

    Dh = NSA_HEAD_DIM
    return {
        "x": normal(ks[0], (BATCH, SEQ, D_MODEL), 1.0),
        "p": normal(ks[1], (DEPTH, BATCH, SEQ, PLE_DIM), 1.0),
        "norm_g": 1.0 + normal(ks[2], (DEPTH, D_MODEL), 0.02),
        "final_norm_g": 1.0 + normal(ks[3], (D_MODEL,), 0.02),
        "ple_proj": normal(ks[4], (DEPTH, PLE_DIM, D_MODEL), PLE_DIM ** -0.5),
        "ple_gate": normal(ks[5], (DEPTH, D_MODEL, D_MODEL), D_MODEL ** -0.5),
        "sb_w_in": normal(ks[6], (N_A, D_MODEL, 4 * SB_WIDTH), D_MODEL ** -0.5),
        "sb_w_out": normal(ks[7], (N_A, SB_WIDTH, D_MODEL), SB_WIDTH ** -0.5),
        "nsa_w_in": normal(ks[8], (N_B, D_MODEL, NSA_IN_DIM), D_MODEL ** -0.5),
        "nsa_cmp_pos_k": normal(ks[9], (N_B, NSA_CMP_LEN, Dh), 0.1),
        "nsa_cmp_pos_v": normal(ks[10], (N_B, NSA_CMP_LEN, Dh), 0.1),
        "nsa_cmp_k_w1": normal(ks[11], (N_B, NSA_CMP_LEN * Dh, Dh), (NSA_CMP_LEN * Dh) ** -0.5),
        "nsa_cmp_k_w2": normal(ks[12], (N_B, Dh, Dh), Dh ** -0.5),
        "nsa_cmp_v_w1": normal(ks[13], (N_B, NSA_CMP_LEN * Dh, Dh), (NSA_CMP_LEN * Dh) ** -0.5),
        "nsa_cmp_v_w2": normal(ks[14], (N_B, Dh, Dh), Dh ** -0.5),
        "nsa_w_out": normal(ks[15], (N_B, NSA_WIDTH, D_MODEL), NSA_WIDTH ** -0.5),
        "gm_w_in": normal(ks[16], (N_C, D_MODEL, 3 * GM_WIDTH), D_MODEL ** -0.5),
        "gm_v_norm_g": 1.0 + normal(ks[17], (N_C, GM_WIDTH), 0.02),
        "gm_w_s": normal(ks[18], (N_C, GM_GROUPS, GM_CHUNK, GM_CHUNK), GM_CHUNK ** -0.5),
        "gm_b_s": 1.0 + normal(ks[19], (N_C, GM_GROUPS, GM_CHUNK), 0.1),
        "gm_w_out": normal(ks[20], (N_C, GM_WIDTH, D_MODEL), GM_WIDTH ** -0.5),
    }


def reference(x, p, norm_g, final_norm_g, ple_proj, ple_gate, sb_w_in, sb_w_out,
              nsa_w_in, nsa_cmp_pos_k, nsa_cmp_pos_v, nsa_cmp_k_w1, nsa_cmp_k_w2,
              nsa_cmp_v_w1, nsa_cmp_v_w2, nsa_w_out,
              gm_w_in, gm_v_norm_g, gm_w_s, gm_b_s, gm_w_out):
    for i in range(DEPTH):
        h = rms_norm(x, norm_g[i])
        kind = i % N_MIXERS
        j = i // N_MIXERS
        if kind == 0:
            y = sb_mixer(h, sb_w_in[j], sb_w_out[j])
        elif kind == 1:
            y = nsa_mixer(h, nsa_w_in[j], nsa_cmp_pos_k[j], nsa_cmp_pos_v[j],
                          nsa_cmp_k_w1[j], nsa_cmp_k_w2[j], nsa_cmp_v_w1[j], nsa_cmp_v_w2[j],
                          nsa_w_out[j])
        else:
            y = gmlp_mixer(h, gm_w_in[j], gm_v_norm_g[j], gm_w_s[j], gm_b_s[j], gm_w_out[j])
        x = x + y.astype(x.dtype)
        x = x + jax.nn.sigmoid(x @ ple_gate[i]) * (p[i] @ ple_proj[i])
    return rms_norm(x, final_norm_g)
```

```python
import numpy as np
import ml_dtypes
import concourse.bass as bass
import concourse.mybir as mybir
from concourse.bass_utils import run_bass_kernel_spmd

F32 = mybir.dt.float32
BF16 = mybir.dt.bfloat16
AF = mybir.ActivationFunctionType
ALU = mybir.AluOpType
AX = mybir.AxisListType


class Tok:
    __slots__ = ("name", "w", "r", "strict")

    def __init__(self, name):
        self.name = name
        self.w = None
        self.r = {}
        self.strict = False


class Buf(Tok):
    __slots__ = ("t",)

    def __init__(self, name, t):
        Tok.__init__(self, name)
        self.t = t


class Prog:
    ENG = ("pe", "act", "dve", "pool", "sp")

    def __init__(self, nc, ndma=8):
        self.nc = nc
        self.eng = {"pe": nc.tensor, "act": nc.scalar, "dve": nc.vector, "pool": nc.gpsimd, "sp": nc.sync}
        self.ops = {e: [] for e in self.ENG}
        self.sem = {e: nc.alloc_semaphore("prg_" + e) for e in self.ENG if e != "sp"}
        self.cnt = {e: 0 for e in self.ENG}
        self.waited = {e: {} for e in self.ENG}
        self.dq = {}
        for q in ("sp", "pool", "act"):
            self.dq[q] = {"sems": [nc.alloc_semaphore("dq_%s_%d" % (q, i)) for i in range(ndma)],
                          "tot": [0] * ndma, "next": 0}
        self.toks = {}
        self.nalloc = 0

    def phase_begin(self):
        import contextlib
        self.stack = contextlib.ExitStack()

    def sb(self, name, shape, dt, strict=False):
        self.nalloc += 1
        b = Buf(name, self.stack.enter_context(self.nc.sbuf_tensor("%s_%d" % (name, self.nalloc), list(shape), dt)))
        b.strict = strict
        return b

    def ps(self, name, shape, dt=F32):
        self.nalloc += 1
        return Buf(name, self.stack.enter_context(self.nc.psum_tensor("%s_%d" % (name, self.nalloc), list(shape), dt)))

    def ring(self, n, name, shape, dt, psum=False, strict=False):
        if psum:
            return [self.ps("%s%d" % (name, i), shape, dt) for i in range(n)]
        return [self.sb("%s%d" % (name, i), shape, dt, strict) for i in range(n)]

    def tok(self, name):
        if name not in self.toks:
            self.toks[name] = Tok(name)
        return self.toks[name]

    def _deps(self, e, reads, writes, mykey):
        deps = {}

        def add(d, strict):
            if d is None:
                return
            k, s, v = d
            if k == mykey and not strict:
                return
            if k not in deps or deps[k][2] < v:
                deps[k] = d
        for t in reads:
            add(t.w, t.strict)
        for t in writes:
            add(t.w, t.strict)
            for d in t.r.values():
                add(d, t.strict)
        waits = []
        wd = self.waited[e]
        for k, (k_, s, v) in deps.items():
            if wd.get(k, 0) >= v:
                continue
            wd[k] = v
            waits.append((s, v))
        return waits

    def _mark(self, dep, reads, writes):
        k = dep[0]
        for t in reads:
            t.r[k] = dep
        for t in writes:
            t.w = dep
            t.r = {}

    def op(self, e, fn, reads=(), writes=()):
        key = "E" + e
        waits = self._deps(e, reads, writes, key)
        self.cnt[e] += 1
        dep = (key, self.sem[e], self.cnt[e])
        self.ops[e].append((waits, fn, (self.sem[e], 1)))
        self._mark(dep, reads, writes)

    def dma(self, q, out_ap, in_ap, reads=(), writes=(), **kw):
        st = self.dq[q]
        slot = st["next"]
        st["next"] = (slot + 1) % len(st["sems"])
        sem = st["sems"][slot]
        key = "D%s%d" % (q, slot)
        waits = self._deps(q, reads, writes, None)
        prev = st["tot"][slot]
        if prev > 0 and self.waited[q].get(key, 0) < prev:
            self.waited[q][key] = prev
            waits.append((sem, prev))
        st["tot"][slot] = prev + 16
        dep = (key, sem, prev + 16)
        self.ops[q].append((waits, lambda eng: eng.dma_start(out=out_ap, in_=in_ap, **kw), (sem, 16)))
        self._mark(dep, reads, writes)

    def phase_end(self):
        allw = [("E" + e, self.sem[e], self.cnt[e]) for e in self.sem if self.cnt[e] > 0]
        for q, st in self.dq.items():
            for slot, sem in enumerate(st["sems"]):
                if st["tot"][slot] > 0:
                    allw.append(("D%s%d" % (q, slot), sem, st["tot"][slot]))
        for e in self.ENG:
            waits = []
            for k, s, v in allw:
                if k == "E" + e:
                    continue
                if self.waited[e].get(k, 0) >= v:
                    continue
                self.waited[e][k] = v
                waits.append((s, v))
            self.ops[e].append((waits, None, None))
        nc = self.nc
        ops = self.ops
        with nc.Block() as block:
            def run(name):
                def body(eng):
                    for waits, fn, inc in ops[name]:
                        for s, v in waits:
                            eng.wait_ge(s, v)
                        if fn is not None:
                            ins = fn(eng)
                            ins.then_inc(inc[0], inc[1])
                return body
            block.tensor(run("pe"))
            block.scalar(run("act"))
            block.vector(run("dve"))
            block.gpsimd(run("pool"))
            block.sync(run("sp"))
        self.ops = {e: [] for e in self.ENG}
        self.stack.close()
        self.stack = None


D = 2048
KC = 16
EPS = 1e-6
PLE = 256
SCALE = 128 ** -0.5
NEGB = -30000.0


def mm(P, ps, lhsT, rhs, start, stop, reads, writes):
    P.op("pe", lambda e: e.matmul(ps, lhsT, rhs, start=start, stop=stop), reads=reads, writes=writes)


def emit_rstd(P, acc, ones, rt, psr, TB, rstd_dram, t0, tokname):
    for tt in range(TB // 512):
        ps = psr[tt % len(psr)]
        sl = slice(tt * 512, (tt + 1) * 512)
        mm(P, ps.t[:], ones.t[:], acc.t[:, sl], True, True, [ones, acc], [ps])
        P.op("dve", lambda e, ps=ps, sl=sl: e.tensor_scalar(out=rt.t[:, sl], in0=ps.t[:], scalar1=1.0 / D, scalar2=EPS,
                                                             op0=ALU.mult, op1=ALU.add), reads=[ps], writes=[rt])
        P.op("act", lambda e, sl=sl: e.activation(out=rt.t[:, sl], in_=rt.t[:, sl], func=AF.Ln), reads=[rt], writes=[rt])
        P.op("act", lambda e, sl=sl: e.activation(out=rt.t[:, sl], in_=rt.t[:, sl], func=AF.Exp, scale=-0.5), reads=[rt], writes=[rt])
    if rstd_dram is not None:
        P.dma("sp", rstd_dram[t0:t0 + TB].rearrange("(o t) -> o t", o=1), rt.t[0:1, :], reads=[rt])


def emit_t0(P, S, xT, gT, hu, rstd):
    NT = S // 4
    TB = min(NT, 1024)
    gt = P.sb("gt", [128, 16], F32)
    ones = P.sb("ones", [128, 128], F32)
    acc = P.sb("acc", [128, TB], F32)
    rt = P.sb("rt", [128, TB], F32)
    xin = P.ring(2, "xin", [128, TB], F32)
    hb = P.ring(2, "hb", [128, TB], BF16)
    sq = P.ring(2, "sq", [128, TB], F32)
    psr = P.ring(2, "psr", [128, 512], F32, psum=True)
    P.dma("sp", gt.t[:], gT, writes=[gt])
    P.op("pool", lambda e: e.memset(ones.t[:], 1.0), writes=[ones])
    for tb in range(NT // TB):
        t0 = tb * TB
        def ld(n):
            P.dma("sp", xin[n % 2].t[:], xT[n * 128:(n + 1) * 128, t0:t0 + TB], writes=[xin[n % 2]])
        ld(0)
        for n in range(KC):
            xi, h, s = xin[n % 2], hb[n % 2], sq[n % 2]
            if n + 1 < KC:
                ld(n + 1)
            P.op("dve", lambda e, xi=xi, h=h, n=n: e.tensor_scalar(out=h.t[:], in0=xi.t[:], scalar1=gt.t[:, n:n + 1], scalar2=None,
                                                                   op0=ALU.mult), reads=[xi, gt], writes=[h])
            P.dma("sp", hu[n * 128:(n + 1) * 128, t0:t0 + TB], h.t[:], reads=[h])
            if n == 0:
                P.op("act", lambda e, xi=xi: e.activation(out=acc.t[:], in_=xi.t[:], func=AF.Square), reads=[xi], writes=[acc])
            else:
                P.op("act", lambda e, xi=xi, s=s: e.activation(out=s.t[:], in_=xi.t[:], func=AF.Square), reads=[xi], writes=[s])
                P.op("pool", lambda e, s=s: e.tensor_tensor(out=acc.t[:], in0=acc.t[:], in1=s.t[:], op=ALU.add), reads=[s, acc], writes=[acc])
        emit_rstd(P, acc, ones, rt, psr, TB, rstd, t0, "rstd")


def emit_tmid(P, S, ao, xT, pT, w_out, w_gate, w_pp, gT, x1s, x2T, hu, rstd, outT=None):
    NT = S // 4
    TB = min(NT, 1024)
    NTT = TB // 512
    A1 = P.sb("A1", [128, KC, TB], BF16)
    A2 = P.sb("A2", [128, KC, TB], BF16)
    pTb = P.sb("pTb", [128, 2, TB], BF16)
    gt = P.sb("gt", [128, 16], F32)
    ones = P.sb("ones", [128, 128], F32)
    acc = P.sb("acc", [128, TB], F32)
    rt = P.sb("rt", [128, TB], F32)
    Wt = P.ring(2, "Wt", [128, KC, 128], BF16)
    Wp = P.ring(2, "Wp", [128, 2, 128], BF16)
    xin = P.ring(2, "xin", [128, TB], F32)
    xo = P.ring(2, "xo", [128, TB], F32)
    hb = P.ring(2, "hb", [128, TB], BF16)
    sg = P.ring(2, "sg", [128, 512], F32)
    sq = P.ring(2, "sq", [128, TB], F32)
    psA = P.ring(3, "psA", [128, 512], F32, psum=True)
    psB = P.ring(2, "psB", [128, 512], F32, psum=True)
    P.dma("sp", gt.t[:], gT, writes=[gt])
    P.op("pool", lambda e: e.memset(ones.t[:], 1.0), writes=[ones])
    wo_v = w_out.rearrange("(c p) n -> p c n", p=128)
    wg_v = w_gate.rearrange("(c p) n -> p c n", p=128)
    wp_v = w_pp.rearrange("(c p) n -> p c n", p=128)
    ia = 0
    for tb in range(NT // TB):
        t0 = tb * TB
        P.dma("sp", A1.t[:], ao[:, t0:t0 + TB].rearrange("(c p) t -> p c t", p=128), writes=[A1])
        P.dma("pool", pTb.t[:], pT[:, t0:t0 + TB].rearrange("(c p) t -> p c t", p=128), writes=[pTb])
        def ld2(n):
            ns = slice(n * 128, (n + 1) * 128)
            P.dma("pool", Wt[n % 2].t[:], wo_v[:, :, ns], writes=[Wt[n % 2]])
            P.dma("sp", xin[n % 2].t[:], xT[ns, t0:t0 + TB], writes=[xin[n % 2]])
        ld2(0)
        for n in range(KC):
            W, xi, o = Wt[n % 2], xin[n % 2], xo[n % 2]
            ns = slice(n * 128, (n + 1) * 128)
            if n + 1 < KC:
                ld2(n + 1)
            for tt in range(NTT):
                sl = slice(tt * 512, (tt + 1) * 512)
                ps = psA[ia % 3]
                ia += 1
                for c in range(KC):
                    mm(P, ps.t[:], W.t[:, c, :], A1.t[:, c, sl], c == 0, c == KC - 1, [W, A1], [ps])
                P.op("dve", lambda e, ps=ps, xi=xi, o=o, sl=sl: e.tensor_tensor(out=o.t[:, sl], in0=ps.t[:], in1=xi.t[:, sl], op=ALU.add),
                     reads=[ps, xi], writes=[o])
                P.op("act", lambda e, o=o, n=n, sl=sl: e.activation(out=A2.t[:, n, sl], in_=o.t[:, sl], func=AF.Copy), reads=[o], writes=[A2])
            P.dma("sp", x1s[ns, t0:t0 + TB], o.t[:], reads=[o], writes=[P.tok("x1s%d" % n)])
        def ld3(n):
            ns = slice(n * 128, (n + 1) * 128)
            P.dma("pool", Wt[n % 2].t[:], wg_v[:, :, ns], writes=[Wt[n % 2]])
            P.dma("pool", Wp[n % 2].t[:], wp_v[:, :, ns], writes=[Wp[n % 2]])
            P.dma("sp", xin[n % 2].t[:], x1s[ns, t0:t0 + TB], reads=[P.tok("x1s%d" % n)], writes=[xin[n % 2]])
        for n in range(KC):
            W, Wq, xi, o, h, s2 = Wt[n % 2], Wp[n % 2], xin[n % 2], xo[n % 2], hb[n % 2], sq[n % 2]
            ns = slice(n * 128, (n + 1) * 128)
            if n == 0:
                ld3(0)
            if n + 1 < KC:
                ld3(n + 1)
            for tt in range(NTT):
                sl = slice(tt * 512, (tt + 1) * 512)
                psg = psA[ia % 3]
                psp = psB[ia % 2]
                s = sg[ia % 2]
                ia += 1
                for c in range(KC):
                    mm(P, psg.t[:], W.t[:, c, :], A2.t[:, c, sl], c == 0, c == KC - 1, [W, A2], [psg])
                for c in range(2):
                    mm(P, psp.t[:], Wq.t[:, c, :], pTb.t[:, c, sl], c == 0, c == 1, [Wq, pTb], [psp])
                P.op("act", lambda e, psg=psg, s=s: e.activation(out=s.t[:], in_=psg.t[:], func=AF.Sigmoid), reads=[psg], writes=[s])
                P.op("dve", lambda e, psp=psp, s=s: e.tensor_tensor(out=s.t[:], in0=s.t[:], in1=psp.t[:], op=ALU.mult), reads=[psp, s], writes=[s])
                P.op("pool", lambda e, s=s, xi=xi, o=o, sl=sl: e.tensor_tensor(out=o.t[:, sl], in0=s.t[:], in1=xi.t[:, sl], op=ALU.add),
                     reads=[s, xi], writes=[o])
            P.dma("sp", x2T[ns, t0:t0 + TB], o.t[:], reads=[o], writes=[P.tok("x2T%d" % n)])
            if outT is None:
                P.op("dve", lambda e, o=o, h=h, n=n: e.tensor_scalar(out=h.t[:], in0=o.t[:], scalar1=gt.t[:, n:n + 1], scalar2=None, op0=ALU.mult),
                     reads=[o, gt], writes=[h])
                P.dma("sp", hu[ns, t0:t0 + TB], h.t[:], reads=[h])
            if n == 0:
                P.op("act", lambda e, o=o: e.activation(out=acc.t[:], in_=o.t[:], func=AF.Square), reads=[o], writes=[acc])
            else:
                P.op("act", lambda e, o=o, s2=s2: e.activation(out=s2.t[:], in_=o.t[:], func=AF.Square), reads=[o], writes=[s2])
                P.op("pool", lambda e, s2=s2: e.tensor_tensor(out=acc.t[:], in0=acc.t[:], in1=s2.t[:], op=ALU.add), reads=[s2, acc], writes=[acc])
        emit_rstd(P, acc, ones, rt, psB, TB, rstd if outT is None else None, t0, "rstd")
        if outT is not None:
            for n in range(KC):
                xi, o = xin[n % 2], xo[n % 2]
                ns = slice(n * 128, (n + 1) * 128)
                P.dma("sp", xi.t[:], x2T[ns, t0:t0 + TB], reads=[P.tok("x2T%d" % n)], writes=[xi])
                P.op("dve", lambda e, xi=xi, o=o, n=n: e.scalar_tensor_tensor(out=o.t[:], in0=xi.t[:], scalar=gt.t[:, n:n + 1], in1=rt.t[:],
                                                                              op0=ALU.mult, op1=ALU.mult), reads=[xi, gt, rt], writes=[o])
                P.dma("sp", outT[ns, t0:t0 + TB], o.t[:], reads=[o])


def emit_sb_proj(P, S, hu_all, rstd_all, wsb, qT, kT, V, szT):
    NTT = S // 512
    Wf = P.sb("Wf", [128, KC, 2048], BF16)
    hut = P.ring(2, "hut", [128, KC, 512], BF16)
    rbc = P.ring(2, "rbc", [128, 512], F32)
    rcol = P.ring(2, "rcol", [128, 4], F32)
    qst = P.ring(2, "qst", [128, 4, 512], BF16)
    kst = P.ring(2, "kst", [128, 4, 512], BF16)
    zst = P.ring(2, "zst", [128, 4, 512], BF16)
    vst = P.ring(2, "vst", [128, 4, 512], BF16)
    ztmp = P.ring(2, "ztmp", [128, 512], F32)
    psr = P.ring(4, "psp", [128, 512], F32, psum=True)
    wv = wsb.rearrange("(c p) n -> p c n", p=128)
    for i in range(4):
        P.dma("pool", Wf.t[:, :, i * 512:(i + 1) * 512], wv[:, :, i * 512:(i + 1) * 512], writes=[Wf])
    hv = hu_all.rearrange("(c p) t -> p c t", p=128)

    def ld(tt):
        sl = slice(tt * 512, (tt + 1) * 512)
        P.dma("sp", hut[tt % 2].t[:], hv[:, :, sl], writes=[hut[tt % 2]])
        P.dma("sp", rbc[tt % 2].t[:], rstd_all[sl].partition_broadcast(128), writes=[rbc[tt % 2]])
        P.dma("sp", rcol[tt % 2].t[:], rstd_all[sl].rearrange("(j p) -> p j", p=128), writes=[rcol[tt % 2]], allow_slow_non_contiguous=True)
    ld(0)
    ip = 0
    for tt in range(NTT):
        sl = slice(tt * 512, (tt + 1) * 512)
        if tt + 1 < NTT:
            ld(tt + 1)
        H, rb, rc = hut[tt % 2], rbc[tt % 2], rcol[tt % 2]
        qs, ks, zs, vs = qst[tt % 2], kst[tt % 2], zst[tt % 2], vst[tt % 2]
        for kind, off, st in (("q", 0, qs), ("k", 512, ks), ("z", 1536, zs)):
            for h in range(4):
                ps = psr[ip % 4]
                ip += 1
                for c in range(KC):
                    mm(P, ps.t[:], Wf.t[:, c, off + h * 128:off + (h + 1) * 128], H.t[:, c, :], c == 0, c == KC - 1, [Wf, H], [ps])
                if kind == "q":
                    P.op("dve", lambda e, ps=ps, st=st, h=h, rb=rb: e.scalar_tensor_tensor(out=st.t[:, h, :], in0=ps.t[:], scalar=SCALE, in1=rb.t[:],
                                                                                         op0=ALU.mult, op1=ALU.mult), reads=[ps, rb], writes=[st])
                elif kind == "k":
                    P.op("dve", lambda e, ps=ps, st=st, h=h, rb=rb: e.tensor_tensor(out=st.t[:, h, :], in0=ps.t[:], in1=rb.t[:], op=ALU.mult),
                         reads=[ps, rb], writes=[st])
                else:
                    zt = ztmp[ip % 2]
                    P.op("dve", lambda e, ps=ps, zt=zt, rb=rb: e.tensor_tensor(out=zt.t[:], in0=ps.t[:], in1=rb.t[:], op=ALU.mult),
                         reads=[ps, rb], writes=[zt])
                    P.op("act", lambda e, zt=zt, st=st, h=h: e.activation(out=st.t[:, h, :], in_=zt.t[:], func=AF.Silu), reads=[zt], writes=[st])
        for j in range(4):
            ps = psr[ip % 4]
            ip += 1
            for c in range(KC):
                mm(P, ps.t[:], H.t[:, c, j * 128:(j + 1) * 128], Wf.t[:, c, 1024:1536], c == 0, c == KC - 1, [Wf, H], [ps])
            P.op("act", lambda e, ps=ps, vs=vs, j=j, rc=rc: e.activation(out=vs.t[:, j, :], in_=ps.t[:], func=AF.Copy, scale=rc.t[:, j:j + 1]),
                 reads=[ps, rc], writes=[vs])
        P.dma("sp", qT[:, :, sl].rearrange("h p t -> p h t"), qs.t[:], reads=[qs])
        P.dma("sp", kT[:, :, sl].rearrange("h p t -> p h t"), ks.t[:], reads=[ks])
        P.dma("sp", szT[:, :, sl].rearrange("h p t -> p h t"), zs.t[:], reads=[zs])
        P.dma("sp", V[sl, :].rearrange("(j p) f -> p j f", p=128), vs.t[:], reads=[vs])


def emit_sb_attn(P, S, qT, kT, V, szT, aoT, cmask):
    NQG = S // 512
    NKB = S // 128
    Ui = P.sb("Ui", [128, 128], BF16)
    on = P.sb("on", [128, 128], BF16)
    msk = P.sb("msk", [128, 4, 512], BF16)
    P.dma("pool", msk.t[:], cmask["sbmask"], writes=[msk])
    P.dma("pool", Ui.t[:], cmask["uincl"], writes=[Ui])
    P.op("pool", lambda e: e.memset(on.t[:], 1.0), writes=[on])
    kb = P.ring(2, "kb", [128, S], BF16)
    vb = P.ring(2, "vb", [128, NKB, 128], BF16)
    qr = P.ring(3, "qr", [128, 512], BF16)
    zr = P.ring(3, "zr", [128, 512], BF16)
    er = P.ring(3, "er", [128, 512], F32)
    spr = P.ring(3, "spr", [128, 512], BF16)
    argr = P.ring(2, "argr", [128, 512], F32)
    wr = P.ring(2, "wr", [128, 512], F32)
    ar = P.ring(3, "ar", [128, 512], BF16)
    Cr = P.ring(2, "Cr", [128, 512], F32)
    ost = P.ring(2, "ost", [128, 512], BF16)
    psZ = P.ring(2, "psZ", [128, 512], F32, psum=True)
    psT = P.ring(2, "psT", [128, 512], F32, psum=True)
    psC = P.ring(2, "psC", [128, 512], F32, psum=True)
    psO = P.ring(2, "psO", [128, 512], F32, psum=True)
    tiles = []
    chains = []
    for h in range(4):
        for qg in range(NQG):
            nblk = 4 * qg + 4
            ci = len(chains)
            chains.append((h, qg))
            for bi, j in enumerate(range(nblk - 1, -1, -1)):
                tiles.append(dict(h=h, qg=qg, j=j, bi=bi, nblk=nblk, ci=ci, i=len(tiles)))

    def load_head(h):
        P.dma("sp", kb[h % 2].t[:], kT[h], writes=[kb[h % 2]])
        P.dma("sp", vb[h % 2].t[:], V[:, h * 128:(h + 1) * 128].rearrange("(j p) d -> p j d", p=128), writes=[vb[h % 2]])

    def load_chain(ci):
        h, qg = chains[ci]
        sl = slice(qg * 512, (qg + 1) * 512)
        P.dma("sp", qr[ci % 3].t[:], qT[h][:, sl], writes=[qr[ci % 3]])
        P.dma("sp", zr[ci % 3].t[:], szT[h][:, sl], writes=[zr[ci % 3]])

    def stA(t):
        i, h, qg, j, bi, ci = t["i"], t["h"], t["qg"], t["j"], t["bi"], t["ci"]
        if bi == 0:
            if ci == 0:
                load_head(0)
                load_chain(0)
            if qg == 0 and h + 1 < 4:
                load_head(h + 1)
            if ci + 1 < len(chains):
                load_chain(ci + 1)
        K, q = kb[h % 2], qr[ci % 3]
        pz, e, sp = psZ[i % 2], er[i % 3], spr[i % 3]
        mm(P, pz.t[:], K.t[:, j * 128:(j + 1) * 128], q.t[:], True, True, [K, q], [pz])
        P.op("act", lambda en: en.activation(out=e.t[:], in_=pz.t[:], func=AF.Exp), reads=[pz], writes=[e])
        if j >= 4 * qg:
            m = j - 4 * qg
            P.op("pool", lambda en: en.tensor_tensor(out=e.t[:], in0=e.t[:], in1=msk.t[:, m, :], op=ALU.mult), reads=[e, msk], writes=[e])
        P.op("act", lambda en: en.activation(out=sp.t[:], in_=e.t[:], func=AF.Ln, bias=1.0), reads=[e], writes=[sp])

    def stB(t):
        i, bi, nblk = t["i"], t["bi"], t["nblk"]
        sp, e = spr[i % 3], er[i % 3]
        pt, pc = psT[i % 2], psC[i % 2]
        arg, w, a = argr[i % 2], wr[i % 2], ar[i % 3]
        Cc, Cn = Cr[bi % 2], Cr[(bi + 1) % 2]
        last = bi == nblk - 1
        mm(P, pt.t[:], Ui.t[:], sp.t[:], True, True, [Ui, sp], [pt])
        if not last:
            mm(P, pc.t[:], on.t[:], sp.t[:], True, True, [on, sp], [pc])
        if bi == 0:
            P.op("dve", lambda en: en.tensor_copy(out=arg.t[:], in_=pt.t[:]), reads=[pt], writes=[arg])
            if not last:
                P.op("dve", lambda en: en.tensor_copy(out=Cn.t[:], in_=pc.t[:]), reads=[pc], writes=[Cn])
        else:
            P.op("dve", lambda en: en.tensor_tensor(out=arg.t[:], in0=pt.t[:], in1=Cc.t[:], op=ALU.add), reads=[pt, Cc], writes=[arg])
            if not last:
                P.op("dve", lambda en: en.tensor_tensor(out=Cn.t[:], in0=pc.t[:], in1=Cc.t[:], op=ALU.add), reads=[pc, Cc], writes=[Cn])
        P.op("act", lambda en: en.activation(out=w.t[:], in_=arg.t[:], func=AF.Exp, scale=-1.0), reads=[arg], writes=[w])
        P.op("pool", lambda en: en.tensor_tensor(out=a.t[:], in0=e.t[:], in1=w.t[:], op=ALU.mult), reads=[e, w], writes=[a])

    def stC(t):
        i, h, qg, j, bi, nblk, ci = t["i"], t["h"], t["qg"], t["j"], t["bi"], t["nblk"], t["ci"]
        a, Vh, po = ar[i % 3], vb[h % 2], psO[ci % 2]
        mm(P, po.t[:], Vh.t[:, j, :], a.t[:], bi == 0, bi == nblk - 1, [Vh, a], [po])
        if bi == nblk - 1:
            o, z = ost[ci % 2], zr[ci % 3]
            P.op("dve", lambda en: en.tensor_tensor(out=o.t[:], in0=po.t[:], in1=z.t[:], op=ALU.mult), reads=[po, z], writes=[o])
            P.dma("sp", aoT[h * 128:(h + 1) * 128, qg * 512:(qg + 1) * 512], o.t[:], reads=[o])

    n = len(tiles)
    for i in range(n + 2):
        if i < n:
            stA(tiles[i])
        if 0 <= i - 1 < n:
            stB(tiles[i - 1])
        if 0 <= i - 2 < n:
            stC(tiles[i - 2])


class Launch:
    def __init__(self):
        self.nc = bass.Bass("TRN2", target_bir_lowering=False)
        self.P = Prog(self.nc)
        self.outs = []

    def inp(self, name, shape, dt):
        return self.nc.dram_tensor(name, list(shape), dt, kind="ExternalInput").ap()

    def out(self, name, shape, dt):
        self.outs.append(name)
        return self.nc.dram_tensor(name, list(shape), dt, kind="ExternalOutput").ap()

    def scr(self, name, shape, dt):
        return self.nc.dram_tensor(name, list(shape), dt).ap()

    def run(self, in_maps):
        res = run_bass_kernel_spmd(self.nc, in_maps, core_ids=list(range(len(in_maps))))
        return res.results


def sb_consts():
    si = np.arange(128)[:, None]
    qi = np.arange(512)[None, :]
    m = np.stack([(i * 128 + si < qi) for i in range(4)], axis=1).astype(np.float32)
    u = (np.arange(128)[:, None] >= np.arange(128)[None, :]).astype(np.float32)
    return {"sbmask": np.ascontiguousarray(m), "uincl": u}


def build_t0(S):
    L = Launch()
    NT = S // 4
    xT = L.inp("xT", [D, NT], F32)
    gT = L.inp("gT", [128, 16], F32)
    hu = L.out("hu", [D, NT], BF16)
    rstd = L.out("rstd", [NT], F32)
    L.P.phase_begin()
    emit_t0(L.P, S, xT, gT, hu, rstd)
    L.P.phase_end()
    return L


def build_sb(S):
    L = Launch()
    hu_all = L.inp("hu_all", [D, S], BF16)
    rstd_all = L.inp("rstd_all", [S], F32)
    wsb = L.inp("wsb", [D, 2048], F32)
    cm = {"sbmask": L.inp("sbmask", [128, 4, 512], F32), "uincl": L.inp("uincl", [128, 128], F32)}
    aoT = L.out("aoT", [512, S], BF16)
    qT = L.scr("qT", [4, 128, S], BF16)
    kT = L.scr("kT", [4, 128, S], BF16)
    szT = L.scr("szT", [4, 128, S], BF16)
    V = L.scr("V", [S, 512], BF16)
    L.P.phase_begin()
    emit_sb_proj(L.P, S, hu_all, rstd_all, wsb, qT, kT, V, szT)
    L.P.phase_end()
    L.P.phase_begin()
    emit_sb_attn(L.P, S, qT, kT, V, szT, aoT, cm)
    L.P.phase_end()
    return L


def build_tmid(S, final=False):
    L = Launch()
    NT = S // 4
    ao = L.inp("ao", [D, NT], BF16)
    xT = L.inp("xT", [D, NT], F32)
    pT = L.inp("pT", [PLE, NT], F32)
    w_out = L.inp("w_out", [D, D], F32)
    w_gate = L.inp("w_gate", [D, D], F32)
    w_pp = L.inp("w_pp", [PLE, D], F32)
    gT = L.inp("gT", [128, 16], F32)
    x1s = L.scr("x1s", [D, NT], F32)
    if final:
        x2T = L.scr("x2T", [D, NT], F32)
        outT = L.out("outT", [D, NT], F32)
        hu = rstd = None
    else:
        x2T = L.out("x2T", [D, NT], F32)
        hu = L.out("hu", [D, NT], BF16)
        rstd = L.out("rstd", [NT], F32)
        outT = None
    L.P.phase_begin()
    emit_tmid(L.P, S, ao, xT, pT, w_out, w_gate, w_pp, gT, x1s, x2T, hu, rstd, outT)
    L.P.phase_end()
    return L


def emit_gmlp(P, S, hu, rstd, w_in, vg, wsT, tril, b_s, yT):
    NT = S // 4
    NTT = NT // 512
    Wv = P.sb("Wv", [128, KC, 2048], BF16)
    Wu = P.ring(2, "Wu", [128, KC, 128], BF16)
    Wz = P.ring(2, "Wz", [128, KC, 128], BF16)
    H = P.sb("H", [128, KC, 512], BF16)
    rb = P.sb("rb", [128, 512], F32)
    rc = P.sb("rc", [128, 4], F32)
    usz = P.sb("usz", [128, KC, 512], BF16)
    gv = P.sb("gv", [128, 2048], F32)
    sqj = P.sb("sqj", [128, 2048], F32)
    vn = P.sb("vn", [128, 4, 2048], BF16)
    vgb = P.sb("vgb", [128, 2048], F32)
    wsf = P.sb("wsf", [128, 16, 128], F32)
    wsb = P.sb("wsb", [128, 16, 128], BF16)
    trl = P.sb("trl", [128, 128], F32)
    bsb = P.sb("bsb", [1, 2048], BF16)
    on1 = P.sb("on1", [1, 128], BF16)
    ssq = P.ring(2, "ssq", [128, 1], F32, strict=True)
    tu = P.ring(2, "tu", [128, 512], F32)
    gu = P.ring(2, "gu", [128, 512], BF16)
    sz = P.ring(2, "sz", [128, 512], BF16)
    yst = P.ring(2, "yst", [128, 512], BF16)
    psr = P.ring(4, "psg", [128, 512], F32, psum=True)
    psm = P.ring(2, "psm", [128, 512], F32, psum=True)
    wv = w_in.rearrange("(c p) n -> p c n", p=128)
    for i in range(4):
        P.dma("pool", Wv.t[:, :, i * 512:(i + 1) * 512], wv[:, :, 2048 + i * 512:2048 + (i + 1) * 512], writes=[Wv])
    P.dma("sp", vgb.t[:], vg.partition_broadcast(128), writes=[vgb])
    P.dma("sp", wsf.t[:], wsT, writes=[wsf])
    P.dma("sp", trl.t[:], tril, writes=[trl])
    P.dma("pool", bsb.t[:], b_s.rearrange("(o g) t -> o (g t)", o=1), writes=[bsb])
    P.op("pool", lambda e: e.memset(on1.t[:], 1.0), writes=[on1])
    for g in range(16):
        P.op("dve", lambda e, g=g: e.tensor_tensor(out=wsb.t[:, g, :], in0=wsf.t[:, g, :], in1=trl.t[:], op=ALU.mult), reads=[wsf, trl], writes=[wsb])
    hv = hu.rearrange("(c p) t -> p c t", p=128)
    ip = 0
    for tt in range(NTT):
        sl = slice(tt * 512, (tt + 1) * 512)
        P.dma("sp", H.t[:], hv[:, :, sl], writes=[H])
        P.dma("sp", rb.t[:], rstd[sl].partition_broadcast(128), writes=[rb])
        P.dma("sp", rc.t[:], rstd[sl].rearrange("(j p) -> p j", p=128), writes=[rc], allow_slow_non_contiguous=True)

        def ldw(n):
            ns = slice(n * 128, (n + 1) * 128)
            P.dma("pool", Wu[n % 2].t[:], wv[:, :, ns], writes=[Wu[n % 2]])
            P.dma("pool", Wz[n % 2].t[:], wv[:, :, 4096 + n * 128:4096 + (n + 1) * 128], writes=[Wz[n % 2]])
        ldw(0)
        for j in range(4):
            for fc in range(4):
                ps = psr[ip % 4]
                ip += 1
                for c in range(KC):
                    mm(P, ps.t[:], H.t[:, c, j * 128:(j + 1) * 128], Wv.t[:, c, fc * 512:(fc + 1) * 512], c == 0, c == KC - 1, [H, Wv], [ps])
                P.op("act", lambda e, ps=ps, fc=fc, j=j: e.activation(out=gv.t[:, fc * 512:(fc + 1) * 512], in_=ps.t[:], func=AF.Gelu, scale=rc.t[:, j:j + 1]),
                     reads=[ps, rc], writes=[gv])
            sq = ssq[j % 2]
            P.op("dve", lambda e: e.tensor_tensor(out=sqj.t[:], in0=gv.t[:], in1=gv.t[:], op=ALU.mult), reads=[gv], writes=[sqj])
            P.op("dve", lambda e, sq=sq: e.reduce_sum(out=sq.t[:], in_=sqj.t[:], axis=AX.X), reads=[sqj], writes=[sq])
            P.op("dve", lambda e, sq=sq: e.tensor_scalar(out=sq.t[:], in0=sq.t[:], scalar1=1.0 / D, scalar2=EPS, op0=ALU.mult, op1=ALU.add), reads=[sq], writes=[sq])
            P.op("act", lambda e, sq=sq: e.activation(out=sq.t[:], in_=sq.t[:], func=AF.Ln), reads=[sq], writes=[sq])
            P.op("act", lambda e, sq=sq: e.activation(out=sq.t[:], in_=sq.t[:], func=AF.Exp, scale=-0.5), reads=[sq], writes=[sq])
            P.op("dve", lambda e, sq=sq, j=j: e.scalar_tensor_tensor(out=vn.t[:, j, :], in0=gv.t[:], scalar=sq.t[:, 0:1], in1=vgb.t[:],
                                                                     op0=ALU.mult, op1=ALU.mult), reads=[gv, sq, vgb], writes=[vn])
        for n in range(KC):
            if n + 1 < KC:
                ldw(n + 1)
            for kind, W in (("u", Wu[n % 2]), ("z", Wz[n % 2])):
                ps = psr[ip % 4]
                t_ = tu[ip % 2]
                ip += 1
                for c in range(KC):
                    mm(P, ps.t[:], W.t[:, c, :], H.t[:, c, :], c == 0, c == KC - 1, [W, H], [ps])
                P.op("dve", lambda e, ps=ps, t_=t_: e.tensor_tensor(out=t_.t[:], in0=ps.t[:], in1=rb.t[:], op=ALU.mult), reads=[ps, rb], writes=[t_])
                if kind == "u":
                    g_ = gu[n % 2]
                    P.op("act", lambda e, t_=t_, g_=g_: e.activation(out=g_.t[:], in_=t_.t[:], func=AF.Gelu), reads=[t_], writes=[g_])
                else:
                    s_ = sz[n % 2]
                    P.op("act", lambda e, t_=t_, s_=s_: e.activation(out=s_.t[:], in_=t_.t[:], func=AF.Silu), reads=[t_], writes=[s_])
            P.op("pool", lambda e, g_=g_, s_=s_, n=n: e.tensor_tensor(out=usz.t[:, n, :], in0=g_.t[:], in1=s_.t[:], op=ALU.mult), reads=[g_, s_], writes=[usz])
        for g in range(16):
            ps = psm[g % 2]
            for j in range(4):
                js = slice(j * 128, (j + 1) * 128)
                mm(P, ps.t[:, js], vn.t[:, j, g * 128:(g + 1) * 128], wsb.t[:, g, :], True, False, [vn, wsb], [ps])
                mm(P, ps.t[:, js], on1.t[0:1, :], bsb.t[0:1, g * 128:(g + 1) * 128], False, True, [on1, bsb], [ps])
            y = yst[g % 2]
            P.op("dve", lambda e, ps=ps, y=y, g=g: e.tensor_tensor(out=y.t[:], in0=ps.t[:], in1=usz.t[:, g, :], op=ALU.mult), reads=[ps, usz], writes=[y])
            P.dma("sp", yT[g * 128:(g + 1) * 128, sl], y.t[:], reads=[y])


def build_gmlp(S):
    L = Launch()
    NT = S // 4
    hu = L.inp("hu", [D, NT], BF16)
    rstd = L.inp("rstd", [NT], F32)
    w_in = L.inp("w_in", [D, 6144], F32)
    vg = L.inp("vg", [2048], F32)
    wsT = L.inp("wsT", [128, 16, 128], F32)
    tril = L.inp("tril", [128, 128], F32)
    b_s = L.inp("b_s", [16, 128], F32)
    yT = L.out("yT", [D, NT], BF16)
    L.P.phase_begin()
    emit_gmlp(L.P, S, hu, rstd, w_in, vg, wsT, tril, b_s, yT)
    L.P.phase_end()
    return L


NSA_COLS = 1804


def nsa_consts(S, g):
    KB = S // 128
    NCP = S // 16
    NCB = NCP // 128
    sl = np.exp2(-8.0 * (np.arange(4 * g, 4 * g + 4) + 1) / 16).astype(np.float64)
    si = np.arange(128)[:, None]
    qi = np.arange(128)[None, :]
    c = {}
    c["negsl"] = np.broadcast_to(-sl[None, :], (128, 4)).astype(np.float32).copy()
    bw = np.zeros((128, 5, 4, 128), np.float64)
    for d in range(5):
        dist = 128 * d + qi - si
        ok = (dist >= 0) & (dist < 512)
        for h in range(4):
            bw[:, d, h, :] = np.where(ok, -sl[h] * dist, NEGB)
    c["BW"] = bw.astype(np.float32)
    c["BSR"] = np.stack([-sl[h] * (qi - si) for h in range(4)], axis=1).astype(np.float32)
    c["cst"] = (-sl[None, None, :] * 128.0 * np.arange(KB)[None, :, None] * np.ones((128, 1, 1))).astype(np.float32)
    c["D0"] = (qi - 16 * si - 31).astype(np.float32)
    j = np.arange(128)[:, None, None]
    kb = np.arange(KB)[None, :, None]
    s = np.arange(128)[None, None, :]
    c["E_all"] = (j == 2 * kb + s // 64).astype(np.float32)
    ci = np.arange(128)[:, None, None]
    cb = np.arange(NCB)[None, :, None]
    jj = np.arange(128)[None, None, :]
    c["Gsel"] = (((cb * 128 + ci) // 4) == jj).astype(np.float32)
    q = np.arange(128)[:, None]
    u = np.arange(256)[None, :] - 128
    cq = (q >= 64).astype(np.int64)
    c["TM"] = (u <= cq - 2).astype(np.float32)
    tb = np.zeros((128, 256), np.float64)
    tb = np.where(u == cq, 2e4, tb)
    tb = np.where(u == cq - 1, 4e4, tb)
    tb = np.where(u > cq, -1e4 - u, tb)
    c["TB"] = tb.astype(np.float32)
    c["ident"] = np.eye(128, dtype=np.float32)
    c["NS"] = np.broadcast_to(-sl[None, :, None], (128, 4, 128)).astype(np.float32).copy()
    return c


NSA_CONST_SHAPES = lambda S: {"negsl": [128, 4], "BW": [128, 5, 4, 128], "BSR": [128, 4, 128], "cst": [128, S // 128, 4], "D0": [128, 128],
                              "E_all": [128, S // 128, 128], "Gsel": [128, S // 2048, 128], "TM": [128, 256], "TB": [128, 256], "ident": [128, 128], "NS": [128, 4, 128]}


def emit_nsa_proj(P, S, hu_all, rstd_all, wns, qT4, kcT, vcT, ksT, kwT, szT, VS, VW, gates):
    NTT = S // 512
    Wf = P.sb("Wf", [128, KC, NSA_COLS], BF16)
    hut = P.ring(2, "hut", [128, KC, 512], BF16)
    rbc = P.ring(2, "rbc", [128, 512], F32)
    rcol = P.ring(2, "rcol", [128, 4], F32)
    qst = P.ring(2, "qst", [128, 4, 512], BF16)
    kst = P.ring(2, "kst", [128, 4, 512], BF16)
    zst = P.ring(2, "zst", [128, 4, 512], BF16)
    vst = P.ring(2, "vst", [128, 4, 256], BF16)
    gst = P.ring(2, "gst", [128, 4, 12], F32)
    ztmp = P.ring(2, "ztmp", [128, 512], F32)
    psr = P.ring(4, "psp", [128, 512], F32, psum=True)
    wv = wns.rearrange("(c p) n -> p c n", p=128)
    for a, b in ((0, 512), (512, 1024), (1024, 1536), (1536, NSA_COLS)):
        P.dma("pool", Wf.t[:, :, a:b], wv[:, :, a:b], writes=[Wf])
    hv = hu_all.rearrange("(c p) t -> p c t", p=128)

    def ld(tt):
        sl = slice(tt * 512, (tt + 1) * 512)
        P.dma("sp", hut[tt % 2].t[:], hv[:, :, sl], writes=[hut[tt % 2]])
        P.dma("sp", rbc[tt % 2].t[:], rstd_all[sl].partition_broadcast(128), writes=[rbc[tt % 2]])
        P.dma("sp", rcol[tt % 2].t[:], rstd_all[sl].rearrange("(j p) -> p j", p=128), writes=[rcol[tt % 2]], allow_slow_non_contiguous=True)
    ld(0)
    ip = 0
    kdst = (kcT, vcT, ksT, kwT)
    for tt in range(NTT):
        sl = slice(tt * 512, (tt + 1) * 512)
        if tt + 1 < NTT:
            ld(tt + 1)
        H, rb, rc = hut[tt % 2], rbc[tt % 2], rcol[tt % 2]
        qs, ks, zs, vs, gs = qst[tt % 2], kst[tt % 2], zst[tt % 2], vst[tt % 2], gst[tt % 2]
        for kind, off, st in (("q", 0, qs), ("k", 512, ks), ("z", 1024, zs)):
            for h in range(4):
                ps = psr[ip % 4]
                ip += 1
                for c in range(KC):
                    mm(P, ps.t[:], Wf.t[:, c, off + h * 128:off + (h + 1) * 128], H.t[:, c, :], c == 0, c == KC - 1, [Wf, H], [ps])
                if kind == "q":
                    P.op("dve", lambda e, ps=ps, st=st, h=h, rb=rb: e.scalar_tensor_tensor(out=st.t[:, h, :], in0=ps.t[:], scalar=SCALE, in1=rb.t[:],
                                                                                         op0=ALU.mult, op1=ALU.mult), reads=[ps, rb], writes=[st])
                elif kind == "k":
                    P.op("dve", lambda e, ps=ps, st=st, h=h, rb=rb: e.tensor_tensor(out=st.t[:, h, :], in0=ps.t[:], in1=rb.t[:], op=ALU.mult),
                         reads=[ps, rb], writes=[st])
                else:
                    zt = ztmp[ip % 2]
                    P.op("dve", lambda e, ps=ps, zt=zt, rb=rb: e.tensor_tensor(out=zt.t[:], in0=ps.t[:], in1=rb.t[:], op=ALU.mult),
                         reads=[ps, rb], writes=[zt])
                    P.op("act", lambda e, zt=zt, st=st, h=h: e.activation(out=st.t[:, h, :], in_=zt.t[:], func=AF.Silu), reads=[zt], writes=[st])
        for j in range(4):
            ps = psr[ip % 4]
            ip += 1
            for c in range(KC):
                mm(P, ps.t[:, 0:268], H.t[:, c, j * 128:(j + 1) * 128], Wf.t[:, c, 1536:NSA_COLS], c == 0, c == KC - 1, [Wf, H], [ps])
            P.op("act", lambda e, ps=ps, vs=vs, j=j, rc=rc: e.activation(out=vs.t[:, j, :], in_=ps.t[:, 0:256], func=AF.Copy, scale=rc.t[:, j:j + 1]),
                 reads=[ps, rc], writes=[vs])
            P.op("act", lambda e, ps=ps, gs=gs, j=j, rc=rc: e.activation(out=gs.t[:, j, :], in_=ps.t[:, 256:268], func=AF.Sigmoid, scale=rc.t[:, j:j + 1]),
                 reads=[ps, rc], writes=[gs])
        for h in range(4):
            P.dma("sp", qT4[:, tt * 4:(tt + 1) * 4, h, :], qs.t[:, h, :].rearrange("p (b q) -> p b q", q=128), reads=[qs])
        for i4 in range(4):
            P.dma("sp", kdst[i4][:, sl], ks.t[:, i4, :], reads=[ks])
        P.dma("sp", szT[:, :, sl].rearrange("h p t -> p h t"), zs.t[:], reads=[zs])
        P.dma("sp", VS[sl, :].rearrange("(j p) d -> p j d", p=128), vs.t[:, :, 0:128], reads=[vs])
        P.dma("sp", VW[sl, :].rearrange("(j p) d -> p j d", p=128), vs.t[:, :, 128:256], reads=[vs])
        P.dma("sp", gates[sl, :].rearrange("(j p) c -> p j c", p=128), gs.t[:], reads=[gs])


def emit_nsa_attn(P, S, qT4, kcT, vcT, ksT, kwT, szT, VS, VW, gates, posk, posv, w1k, w2k, w1v, w2v, cst, aoT, stage=9, kinds="cws"):
    KB = S // 128
    NC = S // 16 - 1
    NCP = S // 16
    NCB = NCP // 128
    negsl = P.sb("negsl", [128, 4], F32)
    BW = P.sb("BW", [128, 5, 4, 128], F32)
    BSR = P.sb("BSR", [128, 4, 128], F32)
    cstt = P.sb("cstt", [128, KB, 4], F32)
    D0 = P.sb("D0", [128, 128], F32)
    E_all = P.sb("E_all", [128, KB, 128], BF16)
    Gsel = P.sb("Gsel", [128, NCB, 128], BF16)
    TM = P.sb("TM", [128, 256], F32)
    TB = P.sb("TB", [128, 256], F32)
    ident = P.sb("ident", [128, 128], F32)
    NSt = P.sb("NSt", [128, 4, 128], F32)
    for nm, t in (("NS", NSt), ("negsl", negsl), ("BW", BW), ("BSR", BSR), ("cst", cstt), ("D0", D0), ("TM", TM), ("TB", TB), ("ident", ident)):
        P.dma("sp", t.t[:], cst[nm], writes=[t])
    P.dma("pool", E_all.t[:], cst["E_all"], writes=[E_all])
    P.dma("pool", Gsel.t[:], cst["Gsel"], writes=[Gsel])
    ksb = P.sb("ksb", [128, S], BF16)
    kwb = P.sb("kwb", [128, S], BF16)
    vsa = P.sb("vsa", [128, KB, 129], BF16)
    vwa = P.sb("vwa", [128, KB, 129], BF16)
    kcm = P.sb("kcm", [128, NCP], BF16)
    vcm = P.sb("vcm", [128, NCB, 129], BF16)
    P.dma("sp", ksb.t[:], ksT, writes=[ksb])
    P.dma("sp", kwb.t[:], kwT, writes=[kwb])
    P.op("pool", lambda e: e.memset(vsa.t[:], 1.0), writes=[vsa])
    P.op("pool", lambda e: e.memset(vwa.t[:], 1.0), writes=[vwa])
    P.op("pool", lambda e: e.memset(vcm.t[:], 1.0), writes=[vcm])
    P.op("pool", lambda e: e.memset(kcm.t[:], 0.0), writes=[kcm])
    P.dma("sp", vsa.t[:, :, 0:128], VS.rearrange("(j p) d -> p j d", p=128), writes=[vsa])
    P.dma("sp", vwa.t[:, :, 0:128], VW.rearrange("(j p) d -> p j d", p=128), writes=[vwa])
    psS = P.ring(2, "psS", [128, 512], F32, psum=True)
    psAc = P.ring(4, "psAc", [128, 512], F32, psum=True)
    psI = P.ps("psI", [128, 512], F32)
    psX = P.ps("psX", [128, 512], F32)
    psXr = [psX] * 4
    ctok = P.sb("ctok", [128, S], BF16)
    w1b = P.sb("w1b", [128, 32, 128], BF16)
    w2b = P.sb("w2b", [128, 128], BF16)
    pos = P.sb("pos", [128, 32], F32)
    hid = P.sb("hid", [128, NCP], BF16)
    rl = P.ring(3, "rl", [128, NCP], BF16)
    for which, srcT, posd, w1, w2 in (("k", kcT, posk, w1k, w2k), ("v", vcT, posv, w1v, w2v)):
        P.dma("sp", ctok.t[:], srcT, writes=[ctok])
        P.dma("pool", w1b.t[:], w1.rearrange("(l d) n -> d l n", d=128), writes=[w1b])
        P.dma("pool", w2b.t[:], w2, writes=[w2b])
        P.dma("sp", pos.t[:], posd, writes=[pos])
        P.op("pool", lambda e: e.memset(hid.t[:], 0.0), writes=[hid])
        c3 = ctok.t[:].rearrange("p (c s) -> p c s", s=16)
        ph = psS[0]
        for l in range(32):
            r = rl[l % 3]
            src = c3[:, 0:NC, l:l + 1] if l < 16 else c3[:, 1:NC + 1, l - 16:l - 15]
            P.op("dve", lambda e, r=r, src=src, l=l: e.tensor_scalar(out=r.t[:, 0:NC].rearrange("p (c o) -> p c o", o=1), in0=src, scalar1=pos.t[:, l:l + 1], scalar2=None, op0=ALU.add),
                 reads=[ctok, pos], writes=[r])
            mm(P, ph.t[:, 0:NC], w1b.t[:, l, :], r.t[:, 0:NC], l == 0, l == 31, [w1b, r], [ph])
        P.op("act", lambda e: e.activation(out=hid.t[:, 0:NC], in_=ph.t[:, 0:NC], func=AF.Gelu), reads=[ph], writes=[hid])
        if which == "k":
            p2 = psS[1]
            mm(P, p2.t[:, 0:NC], w2b.t[:], hid.t[:, 0:NC], True, True, [w2b, hid], [p2])
            P.op("dve", lambda e: e.tensor_copy(out=kcm.t[:, 0:NC], in_=p2.t[:, 0:NC]), reads=[p2], writes=[kcm])
        else:
            for cb in range(NCB):
                p2 = psS[1]
                mm(P, p2.t[:, 0:128], hid.t[:, cb * 128:(cb + 1) * 128], w2b.t[:], True, True, [w2b, hid], [p2])
                P.op("dve", lambda e, cb=cb: e.tensor_copy(out=vcm.t[:, cb, 0:128], in_=p2.t[:, 0:128]), reads=[p2], writes=[vcm])
    if stage <= 2:
        return
    q4r = P.ring(2, "q4r", [128, 512], BF16)
    gtr = P.ring(2, "gtr", [128, 12], F32, strict=True)
    szr = P.ring(2, "szr", [128, 4, 128], BF16)
    argr = P.ring(2, "argr", [128, 4, 128], F32)
    er = P.ring(3, "er", [128, 4, 128], BF16)
    b4r = P.ring(2, "b4r", [128, 4, 128], F32)
    t1 = P.sb("t1", [128, 128], F32, strict=True)
    mk = P.sb("mk", [128, 128], F32, strict=True)
    pen = P.sb("pen", [128, 128], F32, strict=True)
    oacc = P.ring(2, "oacc", [128, 4, 128], F32)
    den4 = P.sb("den4", [128, 4], F32, strict=True)
    rden = P.sb("rden", [128, 4], F32, strict=True)
    coef = P.sb("coef", [128, 4], F32, strict=True)
    imp = P.sb("imp", [128, 128], F32, strict=True)
    impu = P.sb("impu", [128, 512], F32)
    adj = P.sb("adj", [128, 128], F32, strict=True)
    wk = P.sb("wk", [128, 128], F32, strict=True)
    m8a = P.sb("m8a", [128, 8], F32, strict=True)
    m8b = P.sb("m8b", [128, 8], F32, strict=True)
    s01 = P.sb("s01", [128, 128], F32, strict=True)
    selT4 = P.sb("selT4", [128, 4, 128], BF16)
    ost = P.ring(2, "ost", [128, 4, 128], BF16)
    state = {"acc": 0}

    def acc_region(set_i, h):
        bank = psAc[h]
        return bank, bank.t[:, 0:129]

    def finalize(set_i, br, qb, first):
        gt, oa = gtr[qb % 2], oacc[qb % 2]
        for h in range(4):
            bank = psAc[h]
            P.op("dve", lambda e, bank=bank, h=h: e.tensor_copy(out=den4.t[:, h:h + 1], in_=bank.t[:, 128:129]), reads=[bank], writes=[den4])
        P.op("dve", lambda e: e.tensor_scalar(out=rden.t[:], in0=den4.t[:], scalar1=1e-30, scalar2=None, op0=ALU.max), reads=[den4], writes=[rden])
        P.op("dve", lambda e: e.reciprocal(out=rden.t[:], in_=rden.t[:]), reads=[rden], writes=[rden])
        P.op("dve", lambda e: e.tensor_tensor(out=coef.t[:], in0=rden.t[:], in1=gt.t[:, br * 4:br * 4 + 4], op=ALU.mult), reads=[rden, gt], writes=[coef])
        for h in range(4):
            bank, reg = acc_region(set_i, h)
            if first:
                P.op("dve", lambda e, reg=reg, h=h: e.tensor_scalar(out=oa.t[:, h, :], in0=reg[:, 0:128], scalar1=coef.t[:, h:h + 1], scalar2=None, op0=ALU.mult),
                     reads=[bank, coef], writes=[oa])
            else:
                P.op("dve", lambda e, reg=reg, h=h: e.scalar_tensor_tensor(out=oa.t[:, h, :], in0=reg[:, 0:128], scalar=coef.t[:, h:h + 1], in1=oa.t[:, h, :],
                                                                           op0=ALU.mult, op1=ALU.add), reads=[bank, coef, oa], writes=[oa])

    def topk(qb):
        for h in range(4):
            src = impu.t[:, h * 128:(h + 1) * 128]
            if h == 0:
                P.op("dve", lambda e, src=src: e.tensor_scalar(out=imp.t[:], in0=src, scalar1=rden.t[:, 0:1], scalar2=None, op0=ALU.mult),
                     reads=[impu, rden], writes=[imp])
            else:
                P.op("dve", lambda e, src=src, h=h: e.scalar_tensor_tensor(out=imp.t[:], in0=src, scalar=rden.t[:, h:h + 1], in1=imp.t[:],
                                                                           op0=ALU.mult, op1=ALU.add), reads=[impu, rden, imp], writes=[imp])
        v0 = 128 - 2 * qb
        P.op("dve", lambda e: e.tensor_tensor(out=adj.t[:], in0=imp.t[:], in1=TM.t[:, v0:v0 + 128], op=ALU.mult), reads=[imp, TM], writes=[adj])
        P.op("dve", lambda e: e.tensor_tensor(out=adj.t[:], in0=adj.t[:], in1=TB.t[:, v0:v0 + 128], op=ALU.add), reads=[adj, TB], writes=[adj])
        P.op("dve", lambda e: e.tensor_scalar(out=adj.t[:, 0:1], in0=adj.t[:, 0:1], scalar1=1e4, scalar2=None, op0=ALU.add), reads=[adj], writes=[adj])
        P.op("dve", lambda e: e.max(out=m8a.t[:], in_=adj.t[:]), reads=[adj], writes=[m8a])
        P.op("dve", lambda e: e.match_replace(out=wk.t[:], in_to_replace=m8a.t[:], in_values=adj.t[:], imm_value=-1e30), reads=[adj, m8a], writes=[wk])
        P.op("dve", lambda e: e.max(out=m8b.t[:], in_=wk.t[:]), reads=[wk], writes=[m8b])
        P.op("dve", lambda e: e.tensor_scalar(out=s01.t[:], in0=adj.t[:], scalar1=m8b.t[:, 7:8], scalar2=None, op0=ALU.is_ge), reads=[adj, m8b], writes=[s01])
        P.op("dve", lambda e: e.tensor_scalar(out=s01.t[:], in0=s01.t[:], scalar1=-1.0, scalar2=-NEGB, op0=ALU.add, op1=ALU.mult), reads=[s01], writes=[s01])
        mm(P, psX.t[:, 0:128], s01.t[:], ident.t[:], True, True, [s01, ident], [psXr[0]])
        for h in range(4):
            P.op("act", lambda e, h=h: e.activation(out=selT4.t[:, h, :], in_=psX.t[:, 0:128], func=AF.Copy), reads=[psXr[0]], writes=[selT4])

    def tile_A(t):
        kind, qb, x, i = t["kind"], t["qb"], t["x"], t["i"]
        q4 = q4r[qb % 2]
        ps, arg, e = psS[i % 2], argr[i % 2], er[i % 3]
        psf = ps.t[:].rearrange("p (h q) -> p h q", q=128)
        if kind == "c":
            off = 128 * qb - 2048 * x
            full = off - 16 * 127 - 31 >= 0
            b4 = b4r[i % 2]
            P.op("pool", lambda en: en.tensor_scalar(out=t1.t[:], in0=D0.t[:], scalar1=float(off), scalar2=None, op0=ALU.add), reads=[D0], writes=[t1])
            if not full:
                P.op("pool", lambda en: en.tensor_scalar(out=mk.t[:], in0=t1.t[:], scalar1=0.0, scalar2=None, op0=ALU.is_ge), reads=[t1], writes=[mk])
                P.op("pool", lambda en: en.tensor_scalar(out=pen.t[:], in0=mk.t[:], scalar1=-1.0, scalar2=-1e7, op0=ALU.add, op1=ALU.mult), reads=[mk], writes=[pen])
                P.op("pool", lambda en: en.tensor_scalar(out=t1.t[:], in0=t1.t[:], scalar1=0.0, scalar2=None, op0=ALU.max), reads=[t1], writes=[t1])
                P.op("pool", lambda en: en.tensor_tensor(out=t1.t[:], in0=t1.t[:], in1=pen.t[:], op=ALU.add), reads=[t1, pen], writes=[t1])
            for h in range(4):
                P.op("pool", lambda en, h=h: en.tensor_tensor(out=b4.t[:, h, :], in0=t1.t[:], in1=NSt.t[:, h, :], op=ALU.mult), reads=[t1, NSt], writes=[b4])
            mm(P, ps.t[:], kcm.t[:, x * 128:(x + 1) * 128], q4.t[:], True, True, [kcm, q4], [ps])
            P.op("dve", lambda en: en.tensor_tensor(out=arg.t[:], in0=psf, in1=b4.t[:], op=ALU.add), reads=[ps, b4], writes=[arg])
        elif kind == "w":
            d = qb - x
            mm(P, ps.t[:], kwb.t[:, x * 128:(x + 1) * 128], q4.t[:], True, True, [kwb, q4], [ps])
            P.op("dve", lambda en: en.tensor_tensor(out=arg.t[:], in0=psf, in1=BW.t[:, d, :, :], op=ALU.add), reads=[ps, BW], writes=[arg])
        else:
            d = qb - x
            mm(P, ps.t[:], ksb.t[:, x * 128:(x + 1) * 128], q4.t[:], True, False, [ksb, q4], [ps])
            mm(P, ps.t[:], E_all.t[:, x, :], selT4.t[:].rearrange("p h q -> p (h q)"), False, True, [E_all, selT4], [ps])
            import os
            if d == 0 or os.environ.get("DBG", "") == "nod":
                P.op("dve", lambda en: en.tensor_tensor(out=arg.t[:], in0=psf, in1=BW.t[:, 0, :, :], op=ALU.add), reads=[ps, BW], writes=[arg])
            else:
                for h in range(4):
                    P.op("dve", lambda en, h=h: en.scalar_tensor_tensor(out=arg.t[:, h, :], in0=psf[:, h, :], scalar=cstt.t[:, d, h:h + 1], in1=BSR.t[:, h, :],
                                                                        op0=ALU.add, op1=ALU.add), reads=[ps, cstt, BSR], writes=[arg])
        P.op("act", lambda en: en.activation(out=e.t[:], in_=arg.t[:], func=AF.Exp), reads=[arg], writes=[e])

    def tile_B(t):
        kind, qb, x, i, set_i = t["kind"], t["qb"], t["x"], t["i"], t["set"]
        e = er[i % 3]
        first, last = t["first"], t["last"]
        vsrc = {"c": vcm, "w": vwa, "s": vsa}[kind]
        for h in range(4):
            bank, reg = acc_region(set_i, h)
            mm(P, reg, e.t[:, h, :], vsrc.t[:, x, :], first, last, [e, vsrc], [bank])
            if kind == "c":
                mm(P, psI.t[:, h * 128:(h + 1) * 128], e.t[:, h, :], Gsel.t[:, x, :], True, True, [e, Gsel], [psI])
        if kind == "c":
            if first:
                P.op("dve", lambda en: en.tensor_copy(out=impu.t[:], in_=psI.t[:]), reads=[psI], writes=[impu])
            else:
                P.op("dve", lambda en: en.tensor_tensor(out=impu.t[:], in0=impu.t[:], in1=psI.t[:], op=ALU.add), reads=[psI, impu], writes=[impu])
        if last:
            br = {"c": 0, "s": 1, "w": 2}[kind]
            finalize(set_i, br, qb, kind == "c")
            if kind == "c":
                topk(qb)
            import os
            if kind == "s" and os.environ.get("DBG", "") != "noout":
                oa, o, z = oacc[qb % 2], ost[qb % 2], szr[qb % 2]
                for h in range(4):
                    xr = psXr[1 + h % 3]
                    reg = psX.t[:, (1 + h % 3) * 128:(2 + h % 3) * 128]
                    mm(P, reg, oa.t[:, h, :], ident.t[:], True, True, [oa, ident], [xr])
                    P.op("dve", lambda en, h=h, reg=reg: en.tensor_tensor(out=o.t[:, h, :], in0=reg, in1=z.t[:, h, :], op=ALU.mult), reads=[xr, z], writes=[o])
                P.dma("sp", aoT[:, qb * 128:(qb + 1) * 128].rearrange("(h p) t -> p h t", p=128), o.t[:], reads=[o])

    tiles = []
    for qb in range(KB):
        cmax = 8 * qb + 6
        ncb = min(NCB, cmax // 128 + 1)
        groups = [("c", list(range(ncb))), ("w", list(range(max(0, qb - 4), qb + 1))), ("s", list(range(0, qb + 1)))]
        for kind, xs in groups:
            if kind not in kinds:
                continue
            set_i = state["acc"] % 2
            state["acc"] += 1
            for n_, x in enumerate(xs):
                tiles.append(dict(kind=kind, qb=qb, x=x, i=len(tiles), set=set_i, first=(n_ == 0), last=(n_ == len(xs) - 1), qfirst=(kind == "c" and n_ == 0)))

    def load_q(qb):
        P.dma("sp", q4r[qb % 2].t[:], qT4[:, qb, :, :].rearrange("p h q -> p (h q)"), writes=[q4r[qb % 2]])
        P.dma("sp", gtr[qb % 2].t[:], gates[qb * 128:(qb + 1) * 128, :], writes=[gtr[qb % 2]])
        P.dma("sp", szr[qb % 2].t[:], szT[:, :, qb * 128:(qb + 1) * 128].rearrange("h p t -> p h t"), writes=[szr[qb % 2]])

    n = len(tiles)
    load_q(0)
    for i in range(n + 1):
        if i < n:
            tile_A(tiles[i])
        if 0 <= i - 1 < n:
            tile_B(tiles[i - 1])
        if i < n and tiles[i]["qfirst"] and tiles[i]["qb"] + 1 < KB:
            load_q(tiles[i]["qb"] + 1)


def build_nsa(S, stage=9, kinds="cws"):
    L = Launch()
    hu_all = L.inp("hu_all", [D, S], BF16)
    rstd_all = L.inp("rstd_all", [S], F32)
    wns = L.inp("wns", [D, NSA_COLS], F32)
    posk = L.inp("posk", [128, 32], F32)
    posv = L.inp("posv", [128, 32], F32)
    w1k = L.inp("w1k", [4096, 128], F32)
    w2k = L.inp("w2k", [128, 128], F32)
    w1v = L.inp("w1v", [4096, 128], F32)
    w2v = L.inp("w2v", [128, 128], F32)
    cst = {k: L.inp("c_" + k, shp, F32) for k, shp in NSA_CONST_SHAPES(S).items()}
    aoT = L.out("aoT", [512, S], BF16)
    qT4 = L.scr("qT4", [128, S // 128, 4, 128], BF16)
    kcT = L.scr("kcT", [128, S], BF16)
    vcT = L.scr("vcT", [128, S], BF16)
    ksT = L.scr("ksT", [128, S], BF16)
    kwT = L.scr("kwT", [128, S], BF16)
    szT = L.scr("szT", [4, 128, S], BF16)
    VS = L.scr("VS", [S, 128], BF16)
    VW = L.scr("VW", [S, 128], BF16)
    gates = L.scr("gates", [S, 12], F32)
    L.P.phase_begin()
    emit_nsa_proj(L.P, S, hu_all, rstd_all, wns, qT4, kcT, vcT, ksT, kwT, szT, VS, VW, gates)
    L.P.phase_end()
    if stage >= 2:
        L.P.phase_begin()
        emit_nsa_attn(L.P, S, qT4, kcT, vcT, ksT, kwT, szT, VS, VW, gates, posk, posv, w1k, w2k, w1v, w2v, cst, aoT, stage, kinds)
        L.P.phase_end()
    return L


def nsa_core_inputs(inp_w_in, g):
    w = inp_w_in
    cols = [w[:, g * 512:(g + 1) * 512]]
    for o in (2048, 2560, 3072, 4096):
        cols.append(w[:, o + g * 128:o + (g + 1) * 128])
    cols.append(w[:, 5168 + g * 512:5168 + (g + 1) * 512])
    cols.append(w[:, 3584 + g * 128:3584 + (g + 1) * 128])
    cols.append(w[:, 4608 + g * 128:4608 + (g + 1) * 128])
    gc = [5120 + br * 16 + g * 4 + n for br in range(3) for n in range(4)]
    cols.append(w[:, gc])
    return np.ascontiguousarray(np.concatenate(cols, axis=1))


_CACHE = {}


def _get(name, fn):
    if name not in _CACHE:
        _CACHE[name] = fn()
    return _CACHE[name]


def _gT(g):
    return np.ascontiguousarray(np.asarray(g, np.float32).reshape(16, 128).T)


def kernel(x, p, norm_g, final_norm_g, ple_proj, ple_gate, sb_w_in, sb_w_out,
           nsa_w_in, nsa_cmp_pos_k, nsa_cmp_pos_v, nsa_cmp_k_w1, nsa_cmp_k_w2,
           nsa_cmp_v_w1, nsa_cmp_v_w2, nsa_w_out,
           gm_w_in, gm_v_norm_g, gm_w_s, gm_b_s, gm_w_out):
    f = lambda a: np.asarray(a, dtype=np.float32)
    x, p = f(x), f(p)
    B, S, _ = x.shape
    NT = S // 4
    cores = [(c // 4, c % 4) for c in range(8)]
    tsl = lambda r: slice(r * NT, (r + 1) * NT)
    xT = [np.ascontiguousarray(x[b, tsl(r), :].T) for b, r in cores]

    def gather_tok(res, key):
        return [np.concatenate([np.asarray(res[b * 4 + r][key]) for r in range(4)], axis=-1) for b in range(B)]

    def run_tmid(ao_list, xT, layer, w_out, g_next, final=False):
        L = _get("tfinal" if final else "tmid", lambda: build_tmid(S, final))
        maps = []
        for c, (b, r) in enumerate(cores):
            maps.append({"ao": ao_list[c], "xT": xT[c], "pT": np.ascontiguousarray(p[layer, b, tsl(r), :].T), "w_out": f(w_out),
                         "w_gate": f(ple_gate[layer]), "w_pp": f(ple_proj[layer]), "gT": _gT(g_next)})
        return L.run(maps)

    def ao_from_heads(res):
        full = [np.concatenate([np.asarray(res[b * 4 + hg]["aoT"]) for hg in range(4)], axis=0) for b in range(B)]
        return [np.ascontiguousarray(full[b][:, tsl(r)]) for b, r in cores]

    def run_sb(hu_all, rstd_all, w_in):
        L = _get("sb", lambda: build_sb(S))
        cs = sb_consts()
        w_in = f(w_in)
        maps = []
        for b, hg in cores:
            wsb = np.concatenate([w_in[:, o + hg * 512:o + (hg + 1) * 512] for o in (0, 2048, 4096, 6144)], axis=1)
            maps.append({"hu_all": hu_all[b], "rstd_all": rstd_all[b], "wsb": np.ascontiguousarray(wsb), **cs})
        return L.run(maps)

    res = _get("t0", lambda: build_t0(S)).run([{"xT": xT[c], "gT": _gT(norm_g[0])} for c in range(8)])
    hu_all, rstd_all = gather_tok(res, "hu"), gather_tok(res, "rstd")
    res = run_sb(hu_all, rstd_all, sb_w_in[0])
    res = run_tmid(ao_from_heads(res), xT, 0, sb_w_out[0], norm_g[1])
    xT = [np.asarray(r_["x2T"]) for r_ in res]
    hu_all, rstd_all = gather_tok(res, "hu"), gather_tok(res, "rstd")
    L = _get("nsa", lambda: build_nsa(S))
    maps = []
    for b, g in cores:
        m = {"hu_all": hu_all[b], "rstd_all": rstd_all[b], "wns": nsa_core_inputs(f(nsa_w_in[0]), g),
             "posk": np.ascontiguousarray(f(nsa_cmp_pos_k[0]).T), "posv": np.ascontiguousarray(f(nsa_cmp_pos_v[0]).T),
             "w1k": f(nsa_cmp_k_w1[0]), "w2k": f(nsa_cmp_k_w2[0]), "w1v": f(nsa_cmp_v_w1[0]), "w2v": f(nsa_cmp_v_w2[0])}
        for k, v in nsa_consts(S, g).items():
            m["c_" + k] = v
        maps.append(m)
    res = L.run(maps)
    res = run_tmid(ao_from_heads(res), xT, 1, nsa_w_out[0], norm_g[2])
    xT = [np.asarray(r_["x2T"]) for r_ in res]
    L = _get("gmlp", lambda: build_gmlp(S))
    tril = (np.arange(128)[None, :] >= np.arange(128)[:, None]).astype(np.float32)
    maps = [{"hu": np.asarray(res[c]["hu"]), "rstd": np.asarray(res[c]["rstd"]), "w_in": f(gm_w_in[0]), "vg": f(gm_v_norm_g[0]),
             "wsT": np.ascontiguousarray(f(gm_w_s[0]).transpose(2, 0, 1)), "tril": tril, "b_s": f(gm_b_s[0])} for c in range(8)]
    resg = L.run(maps)
    res = run_tmid([np.asarray(resg[c]["yT"]) for c in range(8)], xT, 2, gm_w_out[0], norm_g[3])
    xT = [np.asarray(r_["x2T"]) for r_ in res]
    hu_all, rstd_all = gather_tok(res, "hu"), gather_tok(res, "rstd")
    res = run_sb(hu_all, rstd_all, sb_w_in[1])
    res = run_tmid(ao_from_heads(res), xT, 3, sb_w_out[1], final_norm_g, final=True)
    out = np.empty((B, S, D), np.float32)
    for c, (b, r) in enumerate(cores):
        out[b, tsl(r), :] = np.asarray(res[c]["outT"]).T
    return out
```
